# Optimizing a Trainium2 kernel written in Bass

```python
import jax, jax.numpy as jnp
from jax import lax
import numpy as np

D_MODEL = 2048
BATCH = 16
SEQ = 2048
DEPTH = 4
DEC_BATCH = 32
DEC_SEQ = 16
PAST_LEN = 1024

CHUNK = 64
N_MIXERS = 2
N_FOX = (DEPTH + 1) // 2
N_HGRN = DEPTH // 2
FOX_HEADS = 16
FOX_HEAD_DIM = D_MODEL // FOX_HEADS
FOX_Q_BLOCK = 128
HGRN_HEADS = 16
HGRN_KDIM = D_MODEL // HGRN_HEADS
HGRN_VDIM = D_MODEL // HGRN_HEADS
HGRN_BLOCK = 16
D_FF = ((8 * D_MODEL // 3 + 127) // 128) * 128
DEEPNORM_ALPHA = (2 * DEPTH) ** 0.25
DEEPNORM_BETA = (8 * DEPTH) ** -0.25
LN_EPS = 1e-5
RMS_EPS = 1e-6
NEG_INF = -1e30

kernel_name = 'fox_hgrn2_macaron_deepnorm_stream_step'

F32 = jnp.float32


def layer_norm(x, g, b):
    xf = x.astype(F32)
    mu = jnp.mean(xf, axis=-1, keepdims=True)
    var = jnp.mean(jnp.square(xf - mu), axis=-1, keepdims=True)
    return ((xf - mu) * lax.rsqrt(var + LN_EPS) * g.astype(F32) + b.astype(F32)).astype(x.dtype)


def swiglu_ffn(x, w_up, w_down):
    gate, up = jnp.split(x @ w_up, 2, axis=-1)
    return (jax.nn.silu(gate) * up) @ w_down


def fox_block(q_blk, q_pos, cq_blk, k, v, k_pos, ck_t):
    s = jnp.einsum('bqhd,bkhd->bhqk', q_blk, k) * (FOX_HEAD_DIM ** -0.5)
    s = s + jnp.transpose(cq_blk, (0, 2, 1))[..., None] - ck_t[:, :, None, :]
    mask = k_pos[None, :] <= q_pos[:, None]
    s = jnp.where(mask, s, NEG_INF)
    p = jax.nn.softmax(s, axis=-1)
    return jnp.einsum('bhqk,bkhd->bqhd', p, v)


def fox_mixer(x, w_in, b_f, w_out, past):
    B, T, _ = x.shape
    q, k, v, f_logit = jnp.split(x @ w_in, [D_MODEL, 2 * D_MODEL, 3 * D_MODEL], axis=-1)
    hd = (B, T, FOX_HEADS, FOX_HEAD_DIM)
    q, k, v = q.reshape(hd), k.reshape(hd), v.reshape(hd)
    logf = jax.nn.log_sigmoid((f_logit + b_f).astype(F32))
    if past is None:
        k_all, v_all, logf_all, offset = k, v, logf, 0
    else:
        k_past, v_past, logf_past = past
        k_all = jnp.concatenate([k_past.astype(k.dtype), k], axis=1)
        v_all = jnp.concatenate([v_past.astype(v.dtype), v], axis=1)
        logf_all = jnp.concatenate([logf_past.astype(F32), logf], axis=1)
        offset = k_past.shape[1]
    c = jnp.cumsum(logf_all, axis=1)
    ck_t = jnp.transpose(c, (0, 2, 1))
    cq = c[:, offset:]
    kf, vf, qf = k_all.astype(F32), v_all.astype(F32), q.astype(F32)
    k_pos = jnp.arange(k_all.shape[1])
    q_pos = offset + jnp.arange(T)
    if T % FOX_Q_BLOCK == 0:
        nb = T // FOX_Q_BLOCK
        qb = qf.reshape(B, nb, FOX_Q_BLOCK, FOX_HEADS, FOX_HEAD_DIM).swapaxes(0, 1)
        cqb = cq.reshape(B, nb, FOX_Q_BLOCK, FOX_HEADS).swapaxes(0, 1)
        pb = q_pos.reshape(nb, FOX_Q_BLOCK)
        o = lax.map(lambda a: fox_block(a[0], a[1], a[2], kf, vf, k_pos, ck_t), (qb, pb, cqb))
        o = o.swapaxes(0, 1).reshape(B, T, D_MODEL)
    else:
        o = fox_block(qf, q_pos, cq, kf, vf, k_pos, ck_t).reshape(B, T, D_MODEL)
    y = o.astype(x.dtype) @ w_out
    return y, (k, v, logf.astype(x.dtype))


def hgrn2_recurrence(q, k, v, g, s0):
    B, T, H, K = q.shape
    V = v.shape[-1]
    n = T // HGRN_BLOCK
    def blocks(a):
        return a.reshape(B, n, HGRN_BLOCK, *a.shape[2:]).swapaxes(0, 1)
    mask = jnp.tril(jnp.ones((HGRN_BLOCK, HGRN_BLOCK), dtype=bool))
    mid = HGRN_BLOCK // 2
    def step(S, inp):
        qb, kb, vb, gb = inp
        b = jnp.cumsum(gb, axis=1)
        b_mid = b[:, mid:mid + 1]
        qe = qb * jnp.exp(b - b_mid)
        ke = kb * jnp.exp(b_mid - b)
        A = jnp.einsum('bthk,bshk->bhts', qe, ke)
        A = jnp.where(mask, A, 0.0)
        o = jnp.einsum('bhts,bshv->bthv', A, vb) + jnp.einsum('bthk,bhkv->bthv', qb * jnp.exp(b), S)
        b_last = b[:, -1]
        S = S * jnp.exp(b_last)[..., None] + jnp.einsum('bshk,bshv->bhkv', kb * jnp.exp(b_last[:, None] - b), vb)
        return S, o
    S, o = lax.scan(step, s0, (blocks(q), blocks(k), blocks(v), blocks(g)))
    return o.swapaxes(0, 1).reshape(B, T, H, V), S


def hgrn2_mixer(x, w_in, lb, norm_g, w_out, s0):
    B, T, _ = x.shape
    q, z, i_in, g_out = jnp.split(x @ w_in, 4, axis=-1)
    shp = (B, T, HGRN_HEADS, HGRN_KDIM)
    vshp = (B, T, HGRN_HEADS, HGRN_VDIM)
    lbf = lb.astype(F32).reshape(HGRN_HEADS, HGRN_KDIM)
    zf = z.astype(F32).reshape(shp)
    logf = jnp.logaddexp(jnp.log(lbf), jnp.log1p(-lbf) + jax.nn.log_sigmoid(zf))
    kk = (1.0 - lbf) * jax.nn.sigmoid(-zf)
    qf = jax.nn.silu(q.astype(F32)).reshape(shp) * (HGRN_KDIM ** -0.5)
    vf = i_in.astype(F32).reshape(vshp)
    pad = (-T) % HGRN_BLOCK
    if pad:
        pw = ((0, 0), (0, pad), (0, 0), (0, 0))
        qf, kk, vf, logf = jnp.pad(qf, pw), jnp.pad(kk, pw), jnp.pad(vf, pw), jnp.pad(logf, pw)
    o, S = hgrn2_recurrence(qf, kk, vf, logf, s0.astype(F32))
    o = o[:, :T]
    o = o * lax.rsqrt(jnp.mean(jnp.square(o), axis=-1, keepdims=True) + RMS_EPS) * norm_g.astype(F32)
    o = o * jax.nn.silu(g_out.astype(F32).reshape(vshp))
    y = o.reshape(B, T, D_MODEL).astype(x.dtype) @ w_out
    return y, S.astype(x.dtype)


def setup_inputs(seed: int = 0) -> dict:
    key = jax.random.key(seed)
    ks = jax.random.split(key, 20)
    def nrm(k, shape, scale):
        return jax.random.normal(k, shape, F32) * scale
    d = D_MODEL
    return {
        'x_prompt': nrm(ks[0], (BATCH, SEQ, d), 1.0),
        'x_sample': nrm(ks[1], (DEC_BATCH, DEC_SEQ, d), 1.0),
        'cache_fox_k': nrm(ks[2], (N_FOX, DEC_BATCH, PAST_LEN, FOX_HEADS, FOX_HEAD_DIM), 1.0),
        'cache_fox_v': nrm(ks[3], (N_FOX, DEC_BATCH, PAST_LEN, FOX_HEADS, FOX_HEAD_DIM), 1.0),
        'cache_fox_logf': jax.nn.log_sigmoid(nrm(ks[4], (N_FOX, DEC_BATCH, PAST_LEN, FOX_HEADS), 1.0) + 2.0),
        'state_hgrn': nrm(ks[5], (N_HGRN, DEC_BATCH, HGRN_HEADS, HGRN_KDIM, HGRN_VDIM), 0.5),
        'ln_g': 1.0 + nrm(ks[6], (DEPTH, 3, d), 0.02),
        'ln_b': nrm(ks[7], (DEPTH, 3, d), 0.02),
        'ffn1_up': nrm(ks[8], (DEPTH, d, 2 * D_FF), d ** -0.5 * DEEPNORM_BETA),
        'ffn1_down': nrm(ks[9], (DEPTH, D_FF, d), D_FF ** -0.5 * DEEPNORM_BETA),
        'ffn2_up': nrm(ks[10], (DEPTH, d, 2 * D_FF), d ** -0.5 * DEEPNORM_BETA),
        'ffn2_down': nrm(ks[11], (DEPTH, D_FF, d), D_FF ** -0.5 * DEEPNORM_BETA),
        'fox_w_in': nrm(ks[12], (N_FOX, d, 3 * d + FOX_HEADS), d ** -0.5),
        'fox_b_f': nrm(ks[13], (N_FOX, FOX_HEADS), 0.1),
        'fox_w_out': nrm(ks[14], (N_FOX, d, d), d ** -0.5 * DEEPNORM_BETA),
        'hgrn_w_in': nrm(ks[15], (N_HGRN, d, 4 * d), d ** -0.5),
        'hgrn_lb': 1.0 + nrm(ks[16], (DEPTH, d), 0.02),
        'hgrn_norm_g': 1.0 + nrm(ks[17], (N_HGRN, HGRN_VDIM), 0.02),
        'hgrn_w_out': nrm(ks[18], (N_HGRN, d, d), d ** -0.5 * DEEPNORM_BETA),
    }


def reference(x_prompt, x_sample, cache_fox_k, cache_fox_v, cache_fox_logf, state_hgrn,
              ln_g, ln_b, ffn1_up, ffn1_down, ffn2_up, ffn2_down,
              fox_w_in, fox_b_f, fox_w_out, hgrn_w_in, hgrn_lb, hgrn_norm_g, hgrn_w_out):
    lb_soft = jax.nn.softmax(hgrn_lb.astype(F32), axis=0)
    lb_all = jnp.cumsum(lb_soft, axis=0) - lb_soft[0]

    def run_layer(x, i, fox_past, hgrn_s0):
        j = i // N_MIXERS
        x = layer_norm(DEEPNORM_ALPHA * x + 0.5 * swiglu_ffn(x, ffn1_up[i], ffn1_down[i]), ln_g[i, 0], ln_b[i, 0])
        if i % N_MIXERS == 0:
            m, st = fox_mixer(x, fox_w_in[j], fox_b_f[j], fox_w_out[j], fox_past)
        else:
            m, st = hgrn2_mixer(x, hgrn_w_in[j], lb_all[i], hgrn_norm_g[j], hgrn_w_out[j], hgrn_s0)
        x = layer_norm(DEEPNORM_ALPHA * x + m, ln_g[i, 1], ln_b[i, 1])
        x = layer_norm(DEEPNORM_ALPHA * x + 0.5 * swiglu_ffn(x, ffn2_up[i], ffn2_down[i]), ln_g[i, 2], ln_b[i, 2])
        return x, st

    yp, ys = x_prompt, x_sample
    fox_p, fox_s, hg_p, hg_s = [], [], [], []
    for i in range(DEPTH):
        j = i // N_MIXERS
        if i % N_MIXERS == 0:
            yp, st_p = run_layer(yp, i, None, None)
            ys, st_s = run_layer(ys, i, (cache_fox_k[j], cache_fox_v[j], cache_fox_logf[j]), None)
            fox_p.append(st_p)
            fox_s.append(st_s)
        else:
            s0 = jnp.zeros((x_prompt.shape[0], HGRN_HEADS, HGRN_KDIM, HGRN_VDIM), F32)
            yp, st_p = run_layer(yp, i, None, s0)
            ys, st_s = run_layer(ys, i, None, state_hgrn[j])
            hg_p.append(st_p)
            hg_s.append(st_s)

    fox_k_prompt = jnp.stack([s[0] for s in fox_p])
    fox_v_prompt = jnp.stack([s[1] for s in fox_p])
    fox_logf_prompt = jnp.stack([s[2] for s in fox_p])
    hgrn_state_prompt = jnp.stack(hg_p)
    fox_k_sample = jnp.stack([s[0] for s in fox_s])
    fox_v_sample = jnp.stack([s[1] for s in fox_s])
    fox_logf_sample = jnp.stack([s[2] for s in fox_s])
    hgrn_state_sample = jnp.stack(hg_s)
    return (yp, ys, fox_k_prompt, fox_v_prompt, fox_logf_prompt, hgrn_state_prompt,
            fox_k_sample, fox_v_sample, fox_logf_sample, hgrn_state_sample)
```

```python
import contextlib
import numpy as np
import concourse.bass as bass
import concourse.mybir as mybir
from concourse.bass_utils import run_bass_kernel_spmd

F32 = mybir.dt.float32
BF16 = mybir.dt.bfloat16
AF = mybir.ActivationFunctionType
ALU = mybir.AluOpType

D = 2048
KC = 16
DFF = 5504
FC = 43
H = 16
ALPHA = 8.0 ** 0.25
LN_EPS = 1e-5
RMS_EPS = 1e-6


class Buf:
    __slots__ = ("name", "w", "rs")

    def __init__(self, name):
        self.name = name
        self.w = None
        self.rs = []


class Eng:
    def __init__(self, name):
        self.name = name
        self.count = 0
        self.ops = []
        self.seen = {}


class Prog:
    def __init__(self, nc):
        self.nc = nc
        self.E = {n: Eng(n) for n in ("pe", "act", "dve", "pool", "sp")}
        self.dma_cum = {}
        self.nbuf = 0
        self.allbufs = []
        self.fence = {}

    def buf(self, name=None):
        self.nbuf += 1
        b = Buf(name or f"b{self.nbuf}")
        self.allbufs.append(b)
        return b

    def barrier_all(self):
        f = dict(self.fence)
        for b in self.allbufs:
            evs = list(b.rs)
            if b.w:
                evs.append(b.w)
            for k, v in evs:
                if k == "S_pe":
                    v = min(v, self.E["pe"].count)
                if f.get(k, -1) < v:
                    f[k] = v
            b.rs = []
        for E in self.E.values():
            if E.count:
                k = "S_" + E.name
                if f.get(k, -1) < E.count:
                    f[k] = E.count
        for k, v in self.dma_cum.items():
            if f.get(k, -1) < v:
                f[k] = v
        self.fence = f

    def _collect(self, E, reads, writes, extra=()):
        need = {}

        def add(ev):
            if ev is None:
                return
            k, v = ev
            if need.get(k, -1) < v:
                need[k] = v
        for b in reads:
            add(b.w)
        for b in writes:
            add(b.w)
            for ev in b.rs:
                add(ev)
        for ev in extra:
            add(ev)
        for k, v in self.fence.items():
            if E.seen.get(k, -1) < v:
                add((k, v))
        out = []
        for k, v in need.items():
            if k == "S_pe" and E.name == "pe":
                continue
            if E.seen.get(k, -1) >= v:
                continue
            E.seen[k] = v
            out.append((k, v))
        return out

    def op(self, eng, fn, reads=(), writes=(), signal=True, dma=None):
        E = self.E[eng]
        extra = []
        if dma is not None:
            cum = self.dma_cum.get(dma, 0)
            if cum:
                extra.append((dma, cum))
        waits = self._collect(E, reads, writes, extra)
        if dma is not None:
            cum = self.dma_cum.get(dma, 0) + 16
            self.dma_cum[dma] = cum
            ev = (dma, cum)
            inc = (dma, 16)
        elif signal:
            E.count += 1
            ev = ("S_" + eng, E.count)
            inc = ("S_" + eng, 1)
        else:
            ev = ("S_" + eng, E.count + 1)
            inc = None
        E.ops.append((waits, fn, inc))
        for b in reads:
            b.rs.append(ev)
            if len(b.rs) > 64:
                b.rs = b.rs[-48:] if False else b.rs
        for b in writes:
            b.w = ev
            b.rs = []
        return ev

    def finish(self, final_bufs):
        E = self.E["sp"]
        evs = []
        for b in final_bufs:
            if b.w:
                evs.append(b.w)
            evs += b.rs
        waits = self._collect(E, (), (), evs)
        E.ops.append((waits, None, None))

    def replay(self):
        nc = self.nc
        keys = set()
        for E in self.E.values():
            for waits, fn, inc in E.ops:
                for k, _ in waits:
                    keys.add(k)
                if inc:
                    keys.add(inc[0])
        keys = sorted(keys)
        print("n_sems", len(keys), {n: (E.count, len(E.ops)) for n, E in self.E.items()}, flush=True)
        with contextlib.ExitStack() as st:
            sems = {k: st.enter_context(nc.semaphore(k)) for k in keys}
            block = st.enter_context(nc.Block())

            def run(E, eng):
                for waits, fn, inc in E.ops:
                    for k, v in waits:
                        eng.wait_ge(sems[k], v)
                    if fn is None:
                        continue
                    ins = fn()
                    if inc:
                        ins.then_inc(sems[inc[0]], inc[1])

            @block.sync
            def _(e):
                run(self.E["sp"], e)

            @block.tensor
            def _(e):
                run(self.E["pe"], e)

            @block.scalar
            def _(e):
                run(self.E["act"], e)

            @block.vector
            def _(e):
                run(self.E["dve"], e)

            @block.gpsimd
            def _(e):
                run(self.E["pool"], e)


class Cfg:
    def __init__(self, NPS=2, SEQ=2048, NSS=4, DSEQ=16, PAST=1024, DEPTH=4, WL=4, mixers=True, only=None, skip_ffn=False):
        self.NPS, self.SEQ, self.NSS, self.DSEQ, self.PAST, self.DEPTH = NPS, SEQ, NSS, DSEQ, PAST, DEPTH
        self.TT = 512
        self.WL = WL
        self.mixers = mixers
        self.only = only
        self.skip_ffn = skip_ffn
        self.NFOX = (DEPTH + 1) // 2
        self.NHG = DEPTH // 2
        self.NTS = NSS * DSEQ
        self.tiles = []
        for s in range(NPS):
            for t in range(SEQ // self.TT):
                self.tiles.append(("p", s, t * self.TT, self.TT))
        self.tiles.append(("s", 0, 0, self.NTS))


def build(cfg):
    nc = bass.Bass("TRN2", target_bir_lowering=False)
    c = cfg
    NP_TOK = c.NPS * c.SEQ
    DEPTH = c.DEPTH
    WL = c.WL
    WLM = max(1, (WL + 1) // 2)
    WLH = max(1, WL // 2)
    TT = c.TT
    PAST = c.PAST
    SK_S = PAST + c.DSEQ
    NKT_S = (SK_S + 127) // 128

    def din(name, shape):
        return nc.dram_tensor(name, list(shape), F32, kind="ExternalInput").ap()

    def dout(name, shape):
        return nc.dram_tensor(name, list(shape), F32, kind="ExternalOutput").ap()

    x_prompt = din("x_prompt", [NP_TOK, D])
    x_sample = din("x_sample", [c.NTS, D])
    cache_k = din("cache_fox_k", [c.NFOX * c.NSS * PAST, D])
    cache_v = din("cache_fox_v", [c.NFOX * c.NSS * PAST, D])
    cache_lf = din("cache_fox_logf", [c.NFOX * c.NSS * PAST, H])
    state_h = din("state_hgrn", [max(1, c.NHG) * c.NSS * H * 128, 128])
    ln_g = din("ln_g", [WL, 3, D])
    ln_b = din("ln_b", [WL, 3, D])
    ffn_up = [din("ffn1_up", [WL, D, 2 * DFF]), din("ffn2_up", [WL, D, 2 * DFF])]
    ffn_down = [din("ffn1_down", [WL, DFF, D]), din("ffn2_down", [WL, DFF, D])]
    fox_w_in = din("fox_w_in", [WLM, D, 3 * D + H])
    fox_b_f = din("fox_b_f", [WLM, H])
    fox_w_out = din("fox_w_out", [WLM, D, D])
    hgrn_w_in = din("hgrn_w_in", [WLH, D, 4 * D])
    hgrn_lb = din("hgrn_lb", [4, D])
    hgrn_norm_g = din("hgrn_norm_g", [WLH, 128])
    hgrn_w_out = din("hgrn_w_out", [WLH, D, D])

    y_prompt = dout("y_prompt", [NP_TOK, D])
    y_sample = dout("y_sample", [c.NTS, D])
    o_fk_p = dout("fox_k_prompt", [c.NFOX * NP_TOK, D])
    o_fv_p = dout("fox_v_prompt", [c.NFOX * NP_TOK, D])
    o_fl_p = dout("fox_logf_prompt", [c.NFOX * NP_TOK, H])
    o_hs_p = dout("hgrn_state_prompt", [max(1, c.NHG) * c.NPS * H * 128, 128])
    o_fk_s = dout("fox_k_sample", [c.NFOX * c.NTS, D])
    o_fv_s = dout("fox_v_sample", [c.NFOX * c.NTS, D])
    o_fl_s = dout("fox_logf_sample", [c.NFOX * c.NTS, H])
    o_hs_s = dout("hgrn_state_sample", [max(1, c.NHG) * c.NSS * H * 128, 128])

    NT = len(c.tiles)

    def dscr(name, shape, dt):
        return nc.dram_tensor(name, list(shape), dt, kind="Internal").ap()

    xs = dscr("xs_scratch", [NT, 128, KC * TT], F32)
    os_ = dscr("o_scratch", [NT, 128, KC * TT], BF16)
    qT_s = dscr("qT_s", [H, 128, c.SEQ], BF16)
    kT_s = dscr("kT_s", [H, 128, c.SEQ], BF16)
    v_s = dscr("v_s", [c.SEQ, D], BF16)
    cq_s = dscr("cq_s", [3, H, c.SEQ], BF16)
    qT_ss = dscr("qT_ss", [c.NSS, H, 128, c.DSEQ], BF16)
    kT_ss = dscr("kT_ss", [c.NSS, H, 128, NKT_S * 128], BF16)
    v_ss = dscr("v_ss", [c.NSS, NKT_S * 128, D], BF16)
    cq_ss = dscr("cq_ss", [c.NSS, 3, H, c.DSEQ], BF16)

    P = Prog(nc)
    out_bufs = []
    st = contextlib.ExitStack()
    with st:
        ARENA_W = 52800
        A = st.enter_context(nc.sbuf_tensor("arena", [128, ARENA_W], F32))
        cursor = [0]

        def alloc(nwords):
            o = cursor[0]
            cursor[0] += (nwords + 7) // 8 * 8
            assert cursor[0] <= ARENA_W, cursor[0]
            return o

        def view(off, nwords, dt=F32):
            a = A[:, off:off + nwords]
            return a.bitcast(dt) if dt != F32 else a

        def v3(off, k, t, dt=F32):
            n = k * t if dt == F32 else k * t // 2
            return view(off, n, dt).rearrange("p (k t) -> p k t", k=k)

        o_x32 = [alloc(KC * TT), alloc(KC * TT)]
        o_xbf = alloc(KC * TT // 2)
        NST = 8
        o_wst = alloc(NST * 512)
        NWB = 8
        o_wbf = alloc(NWB * 256)
        o_stat = alloc(6 * 512)
        o_gt = alloc(4 * 512)
        o_const = alloc(128 * 3 + 64 * 2)
        o_lng = alloc(DEPTH * 3 * KC)
        o_lnb = alloc(DEPTH * 3 * KC)
        o_mask = alloc(4 * 256)
        o_nck = alloc(max(c.SEQ // 128, c.NSS * NKT_S) * 16)
        o_wf = alloc(KC * 16 // 2)
        o_small = alloc(64)
        o_lb = alloc(5 * 64 + 4 * KC + 8)
        o_ov = alloc(0)
        OVW = ARENA_W - o_ov
        print("overlay words", OVW, flush=True)

        X32 = [v3(o, KC, TT) for o in o_x32]
        X32_b = [P.buf("x32a"), P.buf("x32b")]
        XBF = v3(o_xbf, KC, TT, BF16)
        XBF_b = P.buf("xbf")
        ACTB = v3(o_ov, FC, TT, BF16)
        ACT_b = P.buf("actb")
        SQ = v3(o_ov, KC, TT)
        STG = view(o_ov, 4 * D).rearrange("p (s f) -> p s f", s=4)
        WST = [view(o_wst + i * 512, 512) for i in range(NST)]
        WST_b = [P.buf(f"wst{i}") for i in range(NST)]
        WBF = [view(o_wbf + i * 256, 256, BF16) for i in range(NWB)]
        WBF_b = [P.buf(f"wbf{i}") for i in range(NWB)]
        STAT = [view(o_stat + i * 512, 512) for i in range(6)]
        STAT_b = [P.buf(f"stat{i}") for i in range(6)]
        GT = v3(o_gt, 4, 512)
        GT_b = P.buf("gt")
        ident = view(o_const, 128)
        ones32 = view(o_const + 128, 128)
        tri32 = view(o_const + 256, 128)
        identbf = view(o_const + 384, 64, BF16)
        onesbf = view(o_const + 448, 64, BF16)
        const_b = P.buf("const")
        LNG = view(o_lng, DEPTH * 3 * KC)
        LNB = view(o_lnb, DEPTH * 3 * KC)
        MASK = v3(o_mask, 4, 512, BF16)
        NCK = view(o_nck, max(c.SEQ // 128, c.NSS * NKT_S) * 16).rearrange("p (k h) -> p k h", h=16)
        NCK_b = P.buf("nck")
        WFBF = v3(o_wf, KC, 16, BF16)
        WF_b = P.buf("wf")
        SMALL = view(o_small, 64)
        SMALL_b = P.buf("small")
        LBE = view(o_lb, 64).rearrange("p (l k) -> p l k", l=4)
        LBT = view(o_lb + 64, 64).rearrange("p (l k) -> p l k", l=4)
        LBR = view(o_lb + 128, 16)
        LBA = view(o_lb + 192, 64).rearrange("p (l k) -> p l k", l=4)
        OML = view(o_lb + 256, 64).rearrange("p (l k) -> p l k", l=4)
        LB_b = P.buf("lb")

        PS = [st.enter_context(nc.psum_tensor(f"ps{i}", [128, 512], F32)) for i in range(8)]
        PSB16 = [p[:, :].bitcast(BF16) for p in PS]
        PS_b = [P.buf(f"ps{i}") for i in range(8)]
        xs_b = [P.buf(f"xs{i}") for i in range(NT)]
        os_b = [P.buf(f"os{i}") for i in range(NT)]

        P.op("pool", lambda: nc.gpsimd.memset(ident, 1.0), writes=[const_b])
        P.op("pool", lambda: nc.gpsimd.affine_select(out=ident, in_=ident, pattern=[[1, 128]], compare_op=ALU.is_equal,
                                                     fill=0.0, base=0, channel_multiplier=-1), reads=[const_b], writes=[const_b])
        P.op("pool", lambda: nc.gpsimd.memset(ones32, 1.0), writes=[const_b])
        P.op("pool", lambda: nc.gpsimd.memset(onesbf, 1.0), writes=[const_b])
        P.op("pool", lambda: nc.gpsimd.tensor_copy(out=identbf, in_=ident), reads=[const_b], writes=[const_b])
        P.op("pool", lambda: nc.gpsimd.affine_select(out=tri32, in_=ones32, pattern=[[1, 128]], compare_op=ALU.is_ge,
                                                     fill=0.0, base=0, channel_multiplier=-1), reads=[const_b], writes=[const_b])
        P.op("pool", lambda: nc.gpsimd.memset(MASK, 0.0), writes=[const_b])
        for o in range(4):
            P.op("pool", (lambda o=o: nc.gpsimd.affine_select(out=MASK[:, o, :], in_=MASK[:, o, :], pattern=[[1, 512]], compare_op=ALU.is_ge,
                                                              fill=-30000.0, base=-128 * o, channel_multiplier=-1)), reads=[const_b], writes=[const_b])
        for (dst, src, key) in ((LNG, ln_g, "Dlng"), (LNB, ln_b, "Dlnb")):
            for l in range(DEPTH):
                for s3 in range(3):
                    o0 = (l * 3 + s3) * KC
                    P.op("sp", (lambda dst=dst, src=src, l=l, s3=s3, o0=o0: nc.sync.dma_start(
                        out=dst[:, o0:o0 + KC], in_=src[l, s3, :].rearrange("(k p) -> p k", p=128),
                        allow_slow_non_contiguous=True)), writes=[const_b], dma=key)
        for l4 in range(4):
            P.op("sp", (lambda l4=l4: nc.sync.dma_start(out=LBE[:, l4, :], in_=hgrn_lb[l4, :].rearrange("(k p) -> p k", p=128),
                                                        allow_slow_non_contiguous=True)), writes=[LB_b], dma="Dlb")
        P.op("act", lambda: nc.scalar.activation(out=LBE, in_=LBE, func=AF.Exp), reads=[LB_b], writes=[LB_b])
        P.op("dve", lambda: nc.vector.tensor_tensor(out=LBR, in0=LBE[:, 0, :], in1=LBE[:, 1, :], op=ALU.add), reads=[LB_b], writes=[LB_b])
        P.op("dve", lambda: nc.vector.tensor_tensor(out=LBR, in0=LBR, in1=LBE[:, 2, :], op=ALU.add), reads=[LB_b], writes=[LB_b])
        P.op("dve", lambda: nc.vector.tensor_tensor(out=LBR, in0=LBR, in1=LBE[:, 3, :], op=ALU.add), reads=[LB_b], writes=[LB_b])
        P.op("dve", lambda: nc.vector.reciprocal(out=LBR, in_=LBR), reads=[LB_b], writes=[LB_b])
        for l4 in range(4):
            P.op("dve", (lambda l4=l4: nc.vector.tensor_tensor(out=LBT[:, l4, :], in0=LBE[:, l4, :], in1=LBR, op=ALU.mult)), reads=[LB_b], writes=[LB_b])
        P.op("dve", lambda: nc.vector.memset(LBA[:, 0, :], 0.0), reads=[LB_b], writes=[LB_b])
        for l4 in range(1, 4):
            P.op("dve", (lambda l4=l4: nc.vector.tensor_tensor(out=LBA[:, l4, :], in0=LBA[:, l4 - 1, :], in1=LBT[:, l4, :], op=ALU.add)), reads=[LB_b], writes=[LB_b])
        P.op("dve", lambda: nc.vector.tensor_scalar(out=OML, in0=LBA, scalar1=-1.0, scalar2=1.0, op0=ALU.mult, op1=ALU.add), reads=[LB_b], writes=[LB_b])

        state = {"piece": 0, "grp": 0}

        def tile_tokens(ti):
            return c.tiles[ti]

        def subtiles(ntok, step=128):
            return [(s0, min(step, ntok - s0)) for s0 in range(0, ntok, step)]

        def smm(wsrc, kchunks, col0, gw, xin, xin_b, ntok, mode, consume, subs=None):
            if mode == "s" and subs is not None and len(subs) > 4:
                banks = list(range(8))
            else:
                ps_set = state["grp"] % 2
                state["grp"] += 1
                banks = [ps_set * 4 + j for j in range(4)]
            nj = (gw + 127) // 128
            if subs is None:
                subs = subtiles(ntok)
            for kc in range(kchunks):
                pi = state["piece"]
                state["piece"] += 1
                s = pi % NST
                wb = pi % NWB
                src = wsrc(kc)[:, col0:col0 + gw]
                P.op("sp", (lambda s=s, src=src: nc.sync.dma_start(out=WST[s][:, 0:gw], in_=src)),
                     writes=[WST_b[s]], dma=f"W{s}")
                if pi % 2 == 0:
                    P.op("act", (lambda s=s, wb=wb: nc.scalar.activation(out=WBF[wb][:, 0:gw], in_=WST[s][:, 0:gw], func=AF.Copy)),
                         reads=[WST_b[s]], writes=[WBF_b[wb]])
                else:
                    P.op("dve", (lambda s=s, wb=wb: nc.vector.tensor_copy(out=WBF[wb][:, 0:gw], in_=WST[s][:, 0:gw])),
                         reads=[WST_b[s]], writes=[WBF_b[wb]])
                first, last = (kc == 0), (kc == kchunks - 1)
                if mode == "n":
                    for j in range(nj):
                        cw = min(128, gw - j * 128)
                        P.op("pe", (lambda j=j, cw=cw, wb=wb, kc=kc, first=first, last=last: nc.tensor.matmul(
                            PS[banks[j]][0:cw, 0:ntok], lhsT=WBF[wb][:, j * 128:j * 128 + cw], rhs=xin[:, kc, 0:ntok],
                            start=first, stop=last)),
                            reads=[WBF_b[wb], xin_b], writes=[PS_b[banks[j]]], signal=(last or j == nj - 1))
                else:
                    for si, (s0, nts) in enumerate(subs):
                        P.op("pe", (lambda si=si, s0=s0, nts=nts, wb=wb, kc=kc, first=first, last=last: nc.tensor.matmul(
                            PS[banks[si]][0:nts, 0:gw], lhsT=xin[:, kc, s0:s0 + nts], rhs=WBF[wb][:, 0:gw],
                            start=first, stop=last)),
                            reads=[WBF_b[wb], xin_b], writes=[PS_b[banks[si]]], signal=(last or si == len(subs) - 1))
            consume(banks)

        def xs3(ti, src=None):
            return (xs if src is None else src)[ti].rearrange("p (k t) -> p k t", k=KC)

        def load_x_from_xs(ti, xb):
            kind, seq, t0, ntok = tile_tokens(ti)
            P.op("sp", lambda: nc.sync.dma_start(out=X32[xb][:, :, 0:ntok], in_=xs3(ti)[:, :, 0:ntok]),
                 reads=[xs_b[ti]], writes=[X32_b[xb]], dma=f"X{xb}")

        def store_x_to_xs(ti, xb):
            kind, seq, t0, ntok = tile_tokens(ti)
            P.op("act", lambda: nc.scalar.dma_start(out=xs3(ti)[:, :, 0:ntok], in_=X32[xb][:, :, 0:ntok]),
                 reads=[X32_b[xb]], writes=[xs_b[ti]], dma=f"X{xb}")

        def tok_src(ti, prompt_ap, sample_ap):
            kind, seq, t0, ntok = tile_tokens(ti)
            if kind == "p":
                return prompt_ap, seq * c.SEQ + t0
            return sample_ap, 0

        def load_x_from_input(ti, xb):
            kind, seq, t0, ntok = tile_tokens(ti)
            src, r0 = tok_src(ti, x_prompt, x_sample)
            subs = subtiles(ntok)
            for si, (s0, nts) in enumerate(subs):
                P.op("sp", (lambda si=si, s0=s0, nts=nts: nc.sync.dma_start(out=STG[0:nts, si, :], in_=src[r0 + s0:r0 + s0 + nts, :])),
                     writes=[ACT_b], dma="Dstg")
            for kc in range(KC):
                bk = kc % 8
                for si, (s0, nts) in enumerate(subs):
                    P.op("pe", (lambda kc=kc, si=si, s0=s0, nts=nts, bk=bk: nc.tensor.transpose(
                        out=PS[bk][:, s0:s0 + nts], in_=STG[0:nts, si, kc * 128:(kc + 1) * 128], identity=ident[0:nts, 0:nts])),
                        reads=[ACT_b, const_b], writes=[PS_b[bk]], signal=(si == len(subs) - 1))
                if kc % 2:
                    P.op("dve", (lambda kc=kc, bk=bk: nc.vector.tensor_copy(out=X32[xb][:, kc, 0:ntok], in_=PS[bk][:, 0:ntok])),
                         reads=[PS_b[bk]], writes=[X32_b[xb]])
                else:
                    P.op("act", (lambda kc=kc, bk=bk: nc.scalar.copy(out=X32[xb][:, kc, 0:ntok], in_=PS[bk][:, 0:ntok])),
                         reads=[PS_b[bk]], writes=[X32_b[xb]])

        def store_x_to_output(ti, xb):
            kind, seq, t0, ntok = tile_tokens(ti)
            dst, r0 = tok_src(ti, y_prompt, y_sample)
            subs = subtiles(ntok)
            ob = P.buf()
            for si, (s0, nts) in enumerate(subs):
                for kg in range(4):
                    bk = (si * 4 + kg) % 8
                    for j in range(4):
                        kc = kg * 4 + j
                        P.op("pe", (lambda kc=kc, j=j, s0=s0, nts=nts, bk=bk: nc.tensor.transpose(
                            out=PS[bk][0:nts, j * 128:(j + 1) * 128], in_=X32[xb][:, kc, s0:s0 + nts], identity=ident)),
                            reads=[X32_b[xb], const_b], writes=[PS_b[bk]], signal=(j == 3))
                    if kg % 2:
                        P.op("dve", (lambda si=si, kg=kg, nts=nts, bk=bk: nc.vector.tensor_copy(
                            out=STG[0:nts, si, kg * 512:(kg + 1) * 512], in_=PS[bk][0:nts, :])), reads=[PS_b[bk]], writes=[ACT_b])
                    else:
                        P.op("act", (lambda si=si, kg=kg, nts=nts, bk=bk: nc.scalar.copy(
                            out=STG[0:nts, si, kg * 512:(kg + 1) * 512], in_=PS[bk][0:nts, :])), reads=[PS_b[bk]], writes=[ACT_b])
                P.op("act", (lambda si=si, s0=s0, nts=nts: nc.scalar.dma_start(out=dst[r0 + s0:r0 + s0 + nts, :], in_=STG[0:nts, si, :])),
                     reads=[ACT_b], writes=[ob], dma="Dstg")
            out_bufs.append(ob)

        def make_xbf(xb, ntok):
            for q in range(4):
                if q % 2 == 0:
                    P.op("act", (lambda q=q: nc.scalar.copy(out=XBF[:, q * 4:(q + 1) * 4, 0:ntok], in_=X32[xb][:, q * 4:(q + 1) * 4, 0:ntok])),
                         reads=[X32_b[xb]], writes=[XBF_b])
                else:
                    P.op("dve", (lambda q=q: nc.vector.tensor_copy(out=XBF[:, q * 4:(q + 1) * 4, 0:ntok], in_=X32[xb][:, q * 4:(q + 1) * 4, 0:ntok])),
                         reads=[X32_b[xb]], writes=[XBF_b])

        def layer_norm(xb, ntok, l, s3, sq=None, sq_b=None):
            g0 = (l * 3 + s3) * KC
            X = X32[xb]
            sq = SQ if sq is None else sq
            sq_b = ACT_b if sq_b is None else sq_b
            for q in range(4):
                P.op("act", (lambda q=q: nc.scalar.activation(out=sq[:, q * 4:(q + 1) * 4, 0:ntok], in_=X[:, q * 4:(q + 1) * 4, 0:ntok], func=AF.Square)),
                     reads=[X32_b[xb]], writes=[sq_b])
            for kc in range(KC):
                P.op("pe", (lambda kc=kc: nc.tensor.matmul(PS[0][:, 0:ntok], lhsT=ones32, rhs=X[:, kc, 0:ntok], start=(kc == 0), stop=(kc == KC - 1))),
                     reads=[X32_b[xb], const_b], writes=[PS_b[0]], signal=(kc == KC - 1))
            for kc in range(KC):
                P.op("pe", (lambda kc=kc: nc.tensor.matmul(PS[1][:, 0:ntok], lhsT=ones32, rhs=sq[:, kc, 0:ntok], start=(kc == 0), stop=(kc == KC - 1))),
                     reads=[sq_b, const_b], writes=[PS_b[1]], signal=(kc == KC - 1))
            mean, msq, var, rstd = (STAT[i][:, 0:ntok] for i in range(4))
            P.op("act", lambda: nc.scalar.mul(out=mean, in_=PS[0][:, 0:ntok], mul=1.0 / D), reads=[PS_b[0]], writes=[STAT_b[0]])
            P.op("dve", lambda: nc.vector.tensor_tensor(out=msq, in0=mean, in1=mean, op=ALU.mult), reads=[STAT_b[0]], writes=[STAT_b[1]])
            P.op("dve", lambda: nc.vector.scalar_tensor_tensor(out=var, in0=PS[1][:, 0:ntok], scalar=1.0 / D, in1=msq, op0=ALU.mult, op1=ALU.subtract),
                 reads=[PS_b[1], STAT_b[1]], writes=[STAT_b[2]])
            P.op("dve", lambda: nc.vector.tensor_scalar(out=var, in0=var, scalar1=LN_EPS, scalar2=None, op0=ALU.add), reads=[STAT_b[2]], writes=[STAT_b[2]])
            P.op("act", lambda: nc.scalar.sqrt(out=var, in_=var), reads=[STAT_b[2]], writes=[STAT_b[2]])
            P.op("dve", lambda: nc.vector.reciprocal(out=rstd, in_=var), reads=[STAT_b[2]], writes=[STAT_b[3]])
            for kc in range(KC):
                P.op("dve", (lambda kc=kc: nc.vector.tensor_tensor(out=X[:, kc, 0:ntok], in0=X[:, kc, 0:ntok], in1=mean, op=ALU.subtract)),
                     reads=[X32_b[xb], STAT_b[0]], writes=[X32_b[xb]])
                P.op("pool", (lambda kc=kc: nc.gpsimd.tensor_tensor(out=X[:, kc, 0:ntok], in0=X[:, kc, 0:ntok], in1=rstd, op=ALU.mult)),
                     reads=[X32_b[xb], STAT_b[3]], writes=[X32_b[xb]])
                P.op("act", (lambda kc=kc: nc.scalar.activation(out=X[:, kc, 0:ntok], in_=X[:, kc, 0:ntok], func=AF.Identity,
                                                               scale=LNG[:, g0 + kc:g0 + kc + 1], bias=LNB[:, g0 + kc:g0 + kc + 1])),
                     reads=[X32_b[xb], const_b], writes=[X32_b[xb]])

        def resid_consumer(xb, ntok, q):
            X = X32[xb]

            def cons(banks):
                for j in range(4):
                    kc = q * 4 + j
                    P.op("dve", (lambda kc=kc, b=banks[j]: nc.vector.scalar_tensor_tensor(
                        out=X[:, kc, 0:ntok], in0=X[:, kc, 0:ntok], scalar=ALPHA, in1=PS[b][:, 0:ntok], op0=ALU.mult, op1=ALU.add)),
                        reads=[PS_b[banks[j]], X32_b[xb]], writes=[X32_b[xb]])
            return cons

        def ffn(xb, ntok, l, which):
            W_up = ffn_up[which]
            W_dn = ffn_down[which]
            make_xbf(xb, ntok)
            ngrp = (FC + 3) // 4
            for g in range(ngrp):
                gw = min(512, DFF - g * 512)
                nj = gw // 128

                def cons_gate(banks, nj=nj):
                    for j in range(nj):
                        P.op("act", (lambda j=j, b=banks[j]: nc.scalar.activation(out=GT[:, j, 0:ntok], in_=PS[b][:, 0:ntok], func=AF.Silu)),
                             reads=[PS_b[banks[j]]], writes=[GT_b])

                def cons_up(banks, nj=nj, g=g):
                    for j in range(nj):
                        P.op("dve", (lambda j=j, b=banks[j]: nc.vector.scalar_tensor_tensor(
                            out=ACTB[:, g * 4 + j, 0:ntok], in0=PS[b][:, 0:ntok], scalar=0.5, in1=GT[:, j, 0:ntok], op0=ALU.mult, op1=ALU.mult)),
                            reads=[PS_b[banks[j]], GT_b], writes=[ACT_b])
                smm(lambda kc: W_up[l, kc * 128:(kc + 1) * 128, :], KC, g * 512, gw, XBF, XBF_b, ntok, "n", cons_gate)
                smm(lambda kc: W_up[l, kc * 128:(kc + 1) * 128, :], KC, DFF + g * 512, gw, XBF, XBF_b, ntok, "n", cons_up)
            for q in range(4):
                smm(lambda kc: W_dn[l, kc * 128:(kc + 1) * 128, :], FC, q * 512, 512, ACTB, ACT_b, ntok, "n", resid_consumer(xb, ntok, q))

        def fox_layer(l):
            j = l // 2
            P.barrier_all()
            oA = o_ov
            KST = view(oA, 8 * 512).rearrange("p (s f) -> p s f", s=8); oA += 4096
            VBS = view(oA, 2048, BF16).rearrange("p (s f) -> p s f", s=8); oA += 2048
            QST = view(oA, 1024, BF16).rearrange("p (s f) -> p s f", s=4); oA += 1024
            LFT = view(oA, 512); oA += 512
            CT = view(oA, 512); oA += 512
            R1 = view(oA, 512); oA += 512
            CHI = view(oA, 256, BF16); oA += 256
            CMI = view(oA, 256, BF16); oA += 256
            CLO = view(oA, 256, BF16); oA += 256
            LFK = view(oA, 64).rearrange("p (s h) -> p s h", s=4); oA += 64
            WFST = view(oA, 256).rearrange("p (k h) -> p k h", k=KC); oA += 256
            STK = view(oA, D); oA += D
            STK16 = view(oA, D // 2, BF16); oA += D // 2
            KTS = view(oA, 512, BF16); oA += 512
            LFP = view(oA, NKT_S * 16).rearrange("p (k h) -> p k h", h=16); oA += NKT_S * 16
            assert oA <= ARENA_W
            KST_b, VBS_b, QST_b, LFT_b, CT_b, R1_b, CS_b, LFK_b, STK_b, STK16_b, KTS_b, LFP_b = [P.buf() for _ in range(12)]
            qs_b, ks_b, vs_b, cqs_b = P.buf("qs"), P.buf("ks"), P.buf("vs"), P.buf("cqs")
            fko_b, fvo_b, flo_b = P.buf(), P.buf(), P.buf()
            out_bufs.extend([fko_b, fvo_b, flo_b])

            P.op("sp", lambda: nc.sync.dma_start(out=WFST, in_=fox_w_in[j, :, 3 * D:3 * D + H].rearrange("(k p) h -> p k h", p=128)),
                 writes=[STK_b], dma="Dwf")
            P.op("dve", lambda: nc.vector.tensor_copy(out=WFBF, in_=WFST), reads=[STK_b], writes=[WF_b])
            P.op("sp", lambda: nc.sync.dma_start(out=SMALL[0:16, 0:1], in_=fox_b_f[j, :].rearrange("(h o) -> h o", o=1)),
                 writes=[SMALL_b], dma="Dwf")
            P.op("dve", lambda: nc.vector.tensor_scalar(out=SMALL[0:16, 0:1], in0=SMALL[0:16, 0:1], scalar1=-1.0, scalar2=None, op0=ALU.mult),
                 reads=[SMALL_b], writes=[SMALL_b])

            def phase_A(ti):
                kind, seq, t0, ntok = tile_tokens(ti)
                xb = 0
                load_x_from_xs(ti, xb)
                make_xbf(xb, ntok)
                W = fox_w_in

                def wsrc(kc):
                    return W[j, kc * 128:(kc + 1) * 128, :]
                for which_qk in range(2):
                    for hg in range(4):
                        def cons(banks, which_qk=which_qk, hg=hg):
                            for jj in range(4):
                                sc = (128.0 ** -0.5) if which_qk == 0 else 1.0
                                P.op("act", (lambda jj=jj, b=banks[jj], sc=sc: nc.scalar.mul(out=QST[:, jj, 0:ntok], in_=PS[b][:, 0:ntok], mul=sc)),
                                     reads=[PS_b[banks[jj]]], writes=[QST_b])
                            if kind == "p":
                                dstT = qT_s if which_qk == 0 else kT_s
                                P.op("act", lambda: nc.scalar.dma_start(out=dstT[hg * 4:(hg + 1) * 4, :, t0:t0 + ntok].rearrange("h p t -> p h t"),
                                                                        in_=QST[:, :, 0:ntok]),
                                     reads=[QST_b], writes=[qs_b if which_qk == 0 else ks_b], dma="Dqst")
                            else:
                                for s in range(c.NSS):
                                    if which_qk == 0:
                                        dst = qT_ss[s, hg * 4:(hg + 1) * 4, :, :]
                                    else:
                                        dst = kT_ss[s, hg * 4:(hg + 1) * 4, :, PAST:PAST + c.DSEQ]
                                    P.op("act", (lambda s=s, dst=dst: nc.scalar.dma_start(out=dst.rearrange("h p t -> p h t"),
                                                                                         in_=QST[:, :, s * c.DSEQ:(s + 1) * c.DSEQ])),
                                         reads=[QST_b], writes=[qs_b if which_qk == 0 else ks_b], dma="Dqst")
                        smm(wsrc, KC, which_qk * D + hg * 512, 512, XBF, XBF_b, ntok, "n", cons)
                import os
                stopA = os.environ.get("FOXSTOPA", "")
                if stopA == "A1":
                    return
                subs = subtiles(ntok) if kind == "p" else [(s * c.DSEQ, c.DSEQ) for s in range(c.NSS)]
                subs_kv = subtiles(ntok, 64) if kind == "p" else subs
                for which_kv in range(int(os.environ.get('NKV', '2'))):
                    for cg in range(4):
                        def cons(banks, which_kv=which_kv, cg=cg):
                            for si, (s0, nts) in enumerate(subs_kv):
                                b = banks[si]
                                P.op("act", (lambda si=si, nts=nts, b=b: nc.scalar.copy(out=KST[0:nts, si, :], in_=PS[b][0:nts, :])),
                                     reads=[PS_b[b]], writes=[KST_b])
                                if which_kv == 1 and os.environ.get("NOVBS", "") != "1":
                                    P.op("act", (lambda si=si, nts=nts, b=b: nc.scalar.copy(out=VBS[0:nts, si, :], in_=PS[b][0:nts, :])),
                                         reads=[PS_b[b]], writes=[VBS_b])
                                if kind == "p":
                                    r0 = j * NP_TOK + seq * c.SEQ + t0 + s0
                                    dsto = (o_fk_p if which_kv == 0 else o_fv_p)[r0:r0 + nts, cg * 512:(cg + 1) * 512]
                                else:
                                    r0 = j * c.NTS + s0
                                    dsto = (o_fk_s if which_kv == 0 else o_fv_s)[r0:r0 + nts, cg * 512:(cg + 1) * 512]
                                if os.environ.get("NOST", "") not in ("1", "3"):
                                    P.op("sp", (lambda si=si, nts=nts, dsto=dsto: nc.sync.dma_start(out=dsto, in_=KST[0:nts, si, :])),
                                         reads=[KST_b], writes=[fko_b if which_kv == 0 else fvo_b], dma="Dkst")
                                if which_kv == 1 and os.environ.get("NOST", "") not in ("2", "3"):
                                    if kind == "p":
                                        dv = v_s[t0 + s0:t0 + s0 + nts, cg * 512:(cg + 1) * 512]
                                    else:
                                        dv = v_ss[si, PAST:PAST + nts, cg * 512:(cg + 1) * 512]
                                    P.op("sp", (lambda si=si, nts=nts, dv=dv: nc.sync.dma_start(out=dv, in_=VBS[0:nts, si, :])),
                                         reads=[VBS_b], writes=[vs_b], dma="Dvbs")
                        smm(wsrc, KC, D + (0 if os.environ.get("VCOLK", "") == "1" else which_kv) * D + cg * 512, 512, XBF, XBF_b, ntok, "s", cons, subs=subs_kv)
                if stopA == "A2":
                    return
                fb = 7
                for kc in range(KC):
                    P.op("pe", (lambda kc=kc: nc.tensor.matmul(PS[fb][0:16, 0:ntok], lhsT=WFBF[:, kc, :], rhs=XBF[:, kc, 0:ntok],
                                                                start=(kc == 0), stop=(kc == KC - 1))),
                         reads=[WF_b, XBF_b], writes=[PS_b[fb]], signal=(kc == KC - 1))
                lft = LFT[0:16, 0:ntok]
                FG = os.environ.get("FG", "")
                if FG == "2":
                    P.op("act", lambda: nc.scalar.copy(out=lft, in_=PS[fb][0:16, 0:ntok]), reads=[PS_b[fb]], writes=[LFT_b])
                else:
                    P.op("act", lambda: nc.scalar.activation(out=lft, in_=PS[fb][0:16, 0:ntok], func=AF.Exp, scale=-1.0, bias=SMALL[0:16, 0:1]),
                         reads=[PS_b[fb], SMALL_b], writes=[LFT_b])
                    P.op("act", lambda: nc.scalar.activation(out=lft, in_=lft, func=AF.Ln, bias=1.0), reads=[LFT_b], writes=[LFT_b])
                    P.op("dve", lambda: nc.vector.tensor_scalar(out=lft, in0=lft, scalar1=-1.0, scalar2=None, op0=ALU.mult), reads=[LFT_b], writes=[LFT_b])
                if FG == "1":
                    return
                for si, (s0, nts) in enumerate(subs):
                    P.op("pe", (lambda s0=s0, nts=nts: nc.tensor.transpose(out=PS[6][0:nts, 0:16], in_=LFT[0:16, s0:s0 + nts], identity=ident[0:16, 0:16])),
                         reads=[LFT_b, const_b], writes=[PS_b[6]])
                    P.op("dve", (lambda si=si, nts=nts: nc.vector.tensor_copy(out=LFK[0:nts, si, :], in_=PS[6][0:nts, 0:16])), reads=[PS_b[6]], writes=[LFK_b])
                    if kind == "p":
                        r0 = j * NP_TOK + seq * c.SEQ + t0 + s0
                        dl = o_fl_p[r0:r0 + nts, :]
                    else:
                        r0 = j * c.NTS + s0
                        dl = o_fl_s[r0:r0 + nts, :]
                    P.op("act", (lambda si=si, nts=nts, dl=dl: nc.scalar.dma_start(out=dl, in_=LFK[0:nts, si, :])), reads=[LFK_b], writes=[flo_b], dma="Dlfk")
                if stopA == "A3":
                    return
                if kind == "p":
                    segs = [(0, ntok, None if t0 == 0 else SMALL[0:16, 1:2], 0)]
                else:
                    segs = [(s * c.DSEQ, c.DSEQ, SMALL[0:16, 8 + s:9 + s], s) for s in range(c.NSS)]
                for (a0, an, init, sidx) in segs:
                    P.op("dve", (lambda a0=a0, an=an, init=init: nc.vector.tensor_tensor_scan(
                        out=CT[0:16, a0:a0 + an], data0=ones32[0:16, 0:an] if an <= 128 else STAT[5][0:16, 0:an], data1=LFT[0:16, a0:a0 + an],
                        initial=(0.0 if init is None else init), op0=ALU.mult, op1=ALU.add)),
                        reads=[LFT_b, SMALL_b, const_b, STAT_b[5]], writes=[CT_b])
                if kind == "p":
                    P.op("dve", lambda: nc.vector.tensor_copy(out=SMALL[0:16, 1:2], in_=CT[0:16, ntok - 1:ntok]), reads=[CT_b], writes=[SMALL_b])
                ct = CT[0:16, 0:ntok]
                chi, cmi, clo, r1 = CHI[0:16, 0:ntok], CMI[0:16, 0:ntok], CLO[0:16, 0:ntok], R1[0:16, 0:ntok]
                P.op("dve", lambda: nc.vector.tensor_copy(out=chi, in_=ct), reads=[CT_b], writes=[CS_b])
                P.op("dve", lambda: nc.vector.tensor_tensor(out=r1, in0=ct, in1=chi, op=ALU.subtract), reads=[CT_b, CS_b], writes=[R1_b])
                P.op("dve", lambda: nc.vector.tensor_copy(out=cmi, in_=r1), reads=[R1_b], writes=[CS_b])
                P.op("dve", lambda: nc.vector.tensor_tensor(out=r1, in0=r1, in1=cmi, op=ALU.subtract), reads=[R1_b, CS_b], writes=[R1_b])
                P.op("dve", lambda: nc.vector.tensor_copy(out=clo, in_=r1), reads=[R1_b], writes=[CS_b])
                for pi_, srcb in enumerate((CHI, CMI, CLO)):
                    if kind == "p":
                        P.op("act", (lambda pi_=pi_, srcb=srcb: nc.scalar.dma_start(out=cq_s[pi_, :, t0:t0 + ntok], in_=srcb[0:16, 0:ntok])),
                             reads=[CS_b], writes=[cqs_b], dma="Dcq")
                    else:
                        for s in range(c.NSS):
                            P.op("act", (lambda pi_=pi_, srcb=srcb, s=s: nc.scalar.dma_start(out=cq_ss[s, pi_, :, :], in_=srcb[0:16, s * c.DSEQ:(s + 1) * c.DSEQ])),
                                 reads=[CS_b], writes=[cqs_b], dma="Dcq")
                if stopA == "A4":
                    return
                for si, (s0, nts) in enumerate(subs):
                    P.op("pe", (lambda s0=s0, nts=nts: nc.tensor.transpose(out=PS[6][0:nts, 16:32], in_=CT[0:16, s0:s0 + nts], identity=ident[0:16, 0:16])),
                         reads=[CT_b, const_b], writes=[PS_b[6]])
                    if kind == "p":
                        dstn = NCK[0:nts, (t0 + s0) // 128, :]
                    else:
                        dstn = NCK[0:nts, si * NKT_S + PAST // 128, :]
                    P.op("act", (lambda nts=nts, dstn=dstn: nc.scalar.mul(out=dstn, in_=PS[6][0:nts, 16:32], mul=-1.0)), reads=[PS_b[6]], writes=[NCK_b])

            def sample_past():
                for s in range(c.NSS):
                    base = (j * c.NSS + s) * PAST
                    for kt in range(PAST // 128):
                        r0 = base + kt * 128
                        P.op("sp", (lambda r0=r0: nc.sync.dma_start(out=STK, in_=cache_k[r0:r0 + 128, :])), writes=[STK_b], dma="Dstk")
                        P.op("act", lambda: nc.scalar.copy(out=STK16, in_=STK), reads=[STK_b], writes=[STK16_b])
                        for half in range(2):
                            bk = 4 + half
                            for hh in range(8):
                                hd = half * 8 + hh
                                P.op("pe", (lambda hd=hd, hh=hh, bk=bk: nc.tensor.transpose(out=PSB16[bk][:, hh * 128:(hh + 1) * 128],
                                                                                          in_=STK16[:, hd * 128:(hd + 1) * 128], identity=identbf)),
                                     reads=[STK16_b, const_b], writes=[PS_b[bk]], signal=(hh == 7))
                            P.op("dve", (lambda bk=bk: nc.vector.tensor_copy(out=KTS, in_=PSB16[bk])), reads=[PS_b[bk]], writes=[KTS_b])
                            P.op("act", (lambda half=half, s=s, kt=kt: nc.scalar.dma_start(
                                out=kT_ss[s, half * 8:(half + 1) * 8, :, kt * 128:(kt + 1) * 128].rearrange("h p t -> p h t"),
                                in_=KTS.rearrange("p (h t) -> p h t", h=8))), reads=[KTS_b], writes=[ks_b], dma="Dkts")
                        P.op("sp", (lambda r0=r0: nc.sync.dma_start(out=STK, in_=cache_v[r0:r0 + 128, :])), writes=[STK_b], dma="Dstk")
                        P.op("dve", lambda: nc.vector.tensor_copy(out=STK16, in_=STK), reads=[STK_b], writes=[STK16_b])
                        P.op("act", (lambda s=s, kt=kt: nc.scalar.dma_start(out=v_ss[s, kt * 128:(kt + 1) * 128, :], in_=STK16)),
                             reads=[STK16_b], writes=[vs_b], dma="Dstk16")
                        P.op("sp", (lambda r0=r0, kt=kt: nc.sync.dma_start(out=LFP[:, kt, :], in_=cache_lf[r0:r0 + 128, :])), writes=[LFP_b], dma="Dlfp")
                    for kt in range(PAST // 128):
                        for k2 in range(kt + 1):
                            P.op("pe", (lambda kt=kt, k2=k2: nc.tensor.matmul(PS[6][:, 0:16], lhsT=(tri32 if k2 == kt else ones32), rhs=LFP[:, k2, :],
                                                                              start=(k2 == 0), stop=(k2 == kt))),
                                 reads=[LFP_b, const_b], writes=[PS_b[6]], signal=(k2 == kt))
                        P.op("act", (lambda s=s, kt=kt: nc.scalar.mul(out=NCK[:, s * NKT_S + kt, :], in_=PS[6][:, 0:16], mul=-1.0)),
                             reads=[PS_b[6]], writes=[NCK_b])
                    for kt in range(PAST // 128):
                        P.op("pe", (lambda kt=kt: nc.tensor.matmul(PS[6][0:16, 32:34], lhsT=LFP[:, kt, :], rhs=ones32[:, 0:2],
                                                                    start=(kt == 0), stop=(kt == PAST // 128 - 1))),
                             reads=[LFP_b, const_b], writes=[PS_b[6]], signal=(kt == PAST // 128 - 1))
                    P.op("dve", (lambda s=s: nc.vector.tensor_copy(out=SMALL[0:16, 8 + s:9 + s], in_=PS[6][0:16, 32:33])), reads=[PS_b[6]], writes=[SMALL_b])

            def phase_B(kind, s):
                P.barrier_all()
                oB = o_ov
                nq = TT if kind == "p" else c.DSEQ
                S = c.SEQ if kind == "p" else NKT_S * 128
                SKv = c.SEQ if kind == "p" else SK_S
                nkt_all = (SKv + 127) // 128
                QH, KH, VH, CQ = [], [], [], []
                qw = (c.SEQ if kind == "p" else c.DSEQ)
                for hb in range(2):
                    QH.append(view(oB, max(8, qw // 2), BF16)); oB += max(8, qw // 2)
                    KH.append(view(oB, S // 2, BF16)); oB += S // 2
                    VH.append(view(oB, nkt_all * 64, BF16).rearrange("p (k d) -> p k d", d=128)); oB += nkt_all * 64
                    CQ.append(view(oB, max(8, qw // 2), BF16)); oB += max(8, qw // 2)
                PBF = []
                for r in range(3):
                    PBF.append(view(oB, 256, BF16)); oB += 256
                RDEN = view(oB, 512); oB += 512
                OST = [view(oB, 256, BF16), view(oB + 256, 256, BF16)]; oB += 512
                assert oB <= ARENA_W, oB
                H_b = [P.buf(), P.buf()]
                PBF_b = [P.buf() for _ in range(3)]
                RDEN_b = P.buf()
                OST_b = [P.buf(), P.buf()]
                cnt = 0
                pair = 0
                for h in range(H):
                    hb = h % 2
                    if kind == "p":
                        srcs = [(QH[hb][:, 0:c.SEQ], qT_s[h]), (KH[hb][:, 0:S], kT_s[h]),
                                (VH[hb], v_s.rearrange("(kt p) f -> p kt f", p=128)[:, :, h * 128:(h + 1) * 128]),
                                (CQ[hb][0:3, 0:c.SEQ], cq_s[:, h, :])]
                    else:
                        srcs = [(QH[hb][:, 0:c.DSEQ], qT_ss[s, h]), (KH[hb][:, 0:S], kT_ss[s, h]),
                                (VH[hb], v_ss[s].rearrange("(kt p) f -> p kt f", p=128)[:, :, h * 128:(h + 1) * 128]),
                                (CQ[hb][0:3, 0:c.DSEQ], cq_ss[s, :, h, :])]
                    for (d_, s_) in srcs:
                        P.op("sp", (lambda d_=d_, s_=s_: nc.sync.dma_start(out=d_, in_=s_)), reads=[qs_b, ks_b, vs_b, cqs_b], writes=[H_b[hb]], dma=f"DH{hb}")
                    nqt = (c.SEQ // TT) if kind == "p" else 1
                    for jq in range(nqt):
                        ob = 4 + (pair % 2) * 2
                        db = ob + 1
                        pair += 1
                        q0 = jq * TT
                        kts = list(range(4 * jq + 4)) if kind == "p" else list(range(nkt_all))
                        for kt in kts:
                            nk = min(128, SKv - kt * 128)
                            sb = cnt % 4
                            r = cnt % 3
                            cnt += 1
                            diag = (kt >= 4 * jq) if kind == "p" else (kt == nkt_all - 1)
                            P.op("pe", (lambda kt=kt, nk=nk, sb=sb, hb=hb, q0=q0: nc.tensor.matmul(
                                PS[sb][0:nk, 0:nq], lhsT=KH[hb][:, kt * 128:kt * 128 + nk], rhs=QH[hb][:, q0:q0 + nq], start=True, stop=False)),
                                reads=[H_b[hb]], writes=[PS_b[sb]], signal=False)
                            P.op("pe", (lambda nk=nk, sb=sb, hb=hb, q0=q0, diag=diag: nc.tensor.matmul(
                                PS[sb][0:nk, 0:nq], lhsT=onesbf[0:3, 0:nk], rhs=CQ[hb][0:3, q0:q0 + nq], start=False, stop=(not diag))),
                                reads=[H_b[hb], const_b], writes=[PS_b[sb]], signal=(not diag))
                            if diag:
                                mo = (kt - 4 * jq) if kind == "p" else 0
                                P.op("pe", (lambda nk=nk, sb=sb, mo=mo: nc.tensor.matmul(
                                    PS[sb][0:nk, 0:nq], lhsT=identbf[0:nk, 0:nk], rhs=MASK[0:nk, mo, 0:nq], start=False, stop=True)),
                                    reads=[const_b], writes=[PS_b[sb]])
                            nckcol = kt if kind == "p" else s * NKT_S + kt
                            P.op("act", (lambda nk=nk, sb=sb, r=r, nckcol=nckcol, h=h: nc.scalar.activation(
                                out=PBF[r][0:nk, 0:nq], in_=PS[sb][0:nk, 0:nq], func=AF.Exp, bias=NCK[0:nk, nckcol, h:h + 1])),
                                reads=[PS_b[sb], NCK_b], writes=[PBF_b[r]])
                            first, last = (kt == kts[0]), (kt == kts[-1])
                            P.op("pe", (lambda nk=nk, r=r, hb=hb, kt=kt, ob=ob, first=first, last=last: nc.tensor.matmul(
                                PS[ob][:, 0:nq], lhsT=VH[hb][0:nk, kt, :], rhs=PBF[r][0:nk, 0:nq], start=first, stop=last)),
                                reads=[H_b[hb], PBF_b[r]], writes=[PS_b[ob]], signal=False)
                            P.op("pe", (lambda nk=nk, r=r, db=db, first=first, last=last: nc.tensor.matmul(
                                PS[db][:, 0:nq], lhsT=onesbf[0:nk, :], rhs=PBF[r][0:nk, 0:nq], start=first, stop=last)),
                                reads=[const_b, PBF_b[r]], writes=[PS_b[db], PS_b[ob]])
                        osb = pair % 2
                        P.op("dve", (lambda db=db: nc.vector.reciprocal(out=RDEN[:, 0:nq], in_=PS[db][:, 0:nq])), reads=[PS_b[db]], writes=[RDEN_b])
                        P.op("dve", (lambda ob=ob, osb=osb: nc.vector.tensor_tensor(out=OST[osb][:, 0:nq], in0=PS[ob][:, 0:nq], in1=RDEN[:, 0:nq], op=ALU.mult)),
                             reads=[PS_b[ob], RDEN_b], writes=[OST_b[osb]])
                        if kind == "p":
                            ti = s * (c.SEQ // TT) + jq
                            dsto = os_[ti][:, h * TT:(h + 1) * TT]
                        else:
                            ti = NT - 1
                            dsto = os_[ti][:, h * TT + s * c.DSEQ:h * TT + (s + 1) * c.DSEQ]
                        P.op("act", (lambda osb=osb, dsto=dsto: nc.scalar.dma_start(out=dsto, in_=OST[osb][:, 0:nq])),
                             reads=[OST_b[osb]], writes=[os_b[ti]], dma=f"Dost{osb}")

            def phase_C(ti):
                kind, seq, t0, ntok = tile_tokens(ti)
                xb = 0
                load_x_from_xs(ti, xb)
                P.op("sp", lambda: nc.sync.dma_start(out=XBF[:, :, 0:ntok], in_=xs3(ti, os_)[:, :, 0:ntok]), reads=[os_b[ti]], writes=[XBF_b], dma="Dxbf")
                for q in range(4):
                    smm(lambda kc: fox_w_out[j, kc * 128:(kc + 1) * 128, :], KC, q * 512, 512, XBF, XBF_b, ntok, "n", resid_consumer(xb, ntok, q))
                layer_norm(xb, ntok, l, 1)
                store_x_to_xs(ti, xb)

            P.op("pool", lambda: nc.gpsimd.memset(STAT[5], 1.0), writes=[STAT_b[5]])
            import os
            stop = os.environ.get("FOXSTOP", "")
            ntp = c.SEQ // TT
            for s in range(c.NPS):
                for t in range(ntp):
                    phase_A(s * ntp + t)
                if stop == "A":
                    continue
                phase_B("p", s)
                P.barrier_all()
            if stop in ("A", "AB"):
                P.barrier_all()
                return
            sample_past()
            if stop == "SP":
                P.barrier_all()
                return
            phase_A(NT - 1)
            if stop == "SA":
                P.barrier_all()
                return
            for s in range(c.NSS):
                phase_B("s", s)
            P.barrier_all()
            if stop == "SB":
                return
            for ti in range(NT):
                phase_C(ti)
            P.barrier_all()

        def hgrn_layer(l):
            jh = l // 2
            P.barrier_all()
            T = [v3(o_x32[1] + i * 2048, 4, 512) for i in range(4)]
            T_b = [P.buf() for _ in range(4)]
            oH = o_ov
            ON = v3(oH, KC, TT, BF16); oH += 4096
            QT4 = v3(oH, 4, 512, BF16); oH += 1024
            KT4 = v3(oH, 4, 512, BF16); oH += 1024
            QH4 = v3(oH, 4, 512, BF16); oH += 1024
            GS4 = v3(oH, 4, 512, BF16); oH += 1024
            VT4 = view(oH, 8 * 256, BF16).rearrange("p (s f) -> p s f", s=8); oH += 2048
            KTT4 = view(oH, 8 * 256, BF16).rearrange("p (s f) -> p s f", s=8); oH += 2048
            O4 = v3(oH, 4, 512); oH += 2048
            S32 = view(oH, H * 128).rearrange("p (h v) -> p h v", h=H); oH += 2048
            SBF = view(oH, H * 64, BF16).rearrange("p (h v) -> p h v", h=H); oH += 1024
            EL = view(oH, 64).rearrange("p (h c) -> p h c", h=4); oH += 64
            ELM = view(oH, 64).rearrange("p (h c) -> p h c", h=4); oH += 64
            ATM = [view(oH, 32, BF16), view(oH + 32, 32, BF16)]; oH += 64
            TMP = [view(oH, 128), view(oH + 128, 128)]; oH += 256
            MRES = STAT[4]
            RR = STAT[3]
            assert oH <= ARENA_W, oH
            ON_b, QT_b, KT_b, QH_b, GS_b, VT_b, KTT_b, O4_b, S32_b, SBF_b, EL_b = [P.buf() for _ in range(11)]
            MRES_b, RR_b = STAT_b[4], STAT_b[3]
            ATM_b = [P.buf(), P.buf()]
            TMP_b = [P.buf(), P.buf()]
            hso_b = P.buf()
            out_bufs.append(hso_b)
            W = hgrn_w_in
            P.op("sp", lambda: nc.sync.dma_start(out=SMALL[:, 2:3], in_=hgrn_norm_g[jh, :].rearrange("(p o) -> p o", o=1)), writes=[SMALL_b], dma="Dwf")

            def wsrc(kc):
                return W[jh, kc * 128:(kc + 1) * 128, :]

            def run_tile(ti):
                kind, seq, t0, ntok = tile_tokens(ti)
                xb = 0
                load_x_from_xs(ti, xb)
                make_xbf(xb, ntok)
                if kind == "p":
                    CL = 64
                    chunks = [(c0, CL, 0) for c0 in range(0, ntok, CL)]
                else:
                    CL = c.DSEQ
                    chunks = [(s * CL, CL, s) for s in range(c.NSS)]
                mid = CL // 2
                nch = len(chunks)
                P.op("pool", lambda: nc.gpsimd.memset(MRES[:, 0:ntok], 1.0), writes=[MRES_b])
                P.op("pool", lambda: nc.gpsimd.memset(MRES[:, 0:ntok].rearrange("p (c l) -> p c l", l=CL)[:, :, 0:1], 0.0), reads=[MRES_b], writes=[MRES_b])
                if kind == "p" and t0 == 0:
                    P.op("pool", lambda: nc.gpsimd.memset(S32, 0.0), writes=[S32_b])
                    P.op("pool", lambda: nc.gpsimd.memset(SBF, 0.0), writes=[SBF_b])
                for hg in range(4):
                    lbc = LBA[:, l, hg * 4:(hg + 1) * 4]
                    def cons_q(banks):
                        for jj in range(4):
                            P.op("act", (lambda jj=jj, b=banks[jj]: nc.scalar.activation(out=T[0][:, jj, 0:ntok], in_=PS[b][:, 0:ntok], func=AF.Silu)),
                                 reads=[PS_b[banks[jj]]], writes=[T_b[0]])

                    def cons_z(banks):
                        for jj in range(4):
                            P.op("act", (lambda jj=jj, b=banks[jj]: nc.scalar.activation(out=T[1][:, jj, 0:ntok], in_=PS[b][:, 0:ntok], func=AF.Sigmoid)),
                                 reads=[PS_b[banks[jj]]], writes=[T_b[1]])

                    def cons_g(banks):
                        for jj in range(4):
                            P.op("act", (lambda jj=jj, b=banks[jj]: nc.scalar.activation(out=GS4[:, jj, 0:ntok], in_=PS[b][:, 0:ntok], func=AF.Silu)),
                                 reads=[PS_b[banks[jj]]], writes=[GS_b])

                    def cons_v(banks):
                        for ci, (c0, cl, sl) in enumerate(chunks):
                            P.op("act", (lambda ci=ci, cl=cl, b=banks[ci]: nc.scalar.copy(out=VT4[0:cl, ci, :], in_=PS[b][0:cl, :])),
                                 reads=[PS_b[banks[ci]]], writes=[VT_b])
                    smm(wsrc, KC, 0 * D + hg * 512, 512, XBF, XBF_b, ntok, "n", cons_q)
                    smm(wsrc, KC, 1 * D + hg * 512, 512, XBF, XBF_b, ntok, "n", cons_z)
                    smm(wsrc, KC, 3 * D + hg * 512, 512, XBF, XBF_b, ntok, "n", cons_g)
                    smm(wsrc, KC, 2 * D + hg * 512, 512, XBF, XBF_b, ntok, "s", cons_v, subs=[(c0, cl) for (c0, cl, sl) in chunks])
                    for jj in range(4):
                        hcol = hg * 4 + jj
                        t0v, t1v, t2v, t3v = (T[i][:, jj, 0:ntok] for i in range(4))
                        lbp = LBA[:, l, hcol:hcol + 1]
                        omp = OML[:, l, hcol:hcol + 1]
                        P.op("dve", (lambda t1v=t1v, omp=omp, lbp=lbp: nc.vector.tensor_scalar(out=t1v, in0=t1v, scalar1=omp, scalar2=lbp, op0=ALU.mult, op1=ALU.add)),
                             reads=[T_b[1], LB_b], writes=[T_b[1]])
                        P.op("act", (lambda t1v=t1v, t2v=t2v: nc.scalar.activation(out=t2v, in_=t1v, func=AF.Ln)), reads=[T_b[1]], writes=[T_b[2]])
                        P.op("pool", (lambda t1v=t1v: nc.gpsimd.tensor_scalar(out=t1v, in0=t1v, scalar1=-1.0, scalar2=1.0, op0=ALU.mult, op1=ALU.add)),
                             reads=[T_b[1], T_b[2]], writes=[T_b[1]])
                        P.op("dve", (lambda t2v=t2v: nc.vector.tensor_tensor_scan(out=t2v, data0=MRES[:, 0:ntok], data1=t2v, initial=0.0, op0=ALU.mult, op1=ALU.add)),
                             reads=[T_b[2], MRES_b], writes=[T_b[2]])
                        b3 = T[2][:, jj, 0:ntok].rearrange("p (c l) -> p c l", l=CL)
                        d3 = T[3][:, jj, 0:ntok].rearrange("p (c l) -> p c l", l=CL)
                        P.op("dve", (lambda b3=b3, d3=d3: nc.vector.tensor_tensor(out=d3, in0=b3, in1=b3[:, :, mid:mid + 1].to_broadcast([128, nch, CL]), op=ALU.subtract)),
                             reads=[T_b[2]], writes=[T_b[3]])
                        P.op("act", (lambda d3=d3, jj=jj: nc.scalar.activation(out=ELM[:, jj, 0:nch], in_=d3[:, :, CL - 1], func=AF.Exp)), reads=[T_b[3]], writes=[EL_b])
                        P.op("act", (lambda b3=b3, jj=jj: nc.scalar.activation(out=EL[:, jj, 0:nch], in_=b3[:, :, CL - 1], func=AF.Exp)), reads=[T_b[2]], writes=[EL_b])
                        P.op("act", (lambda t3v=t3v: nc.scalar.activation(out=t3v, in_=t3v, func=AF.Exp)), reads=[T_b[3]], writes=[T_b[3]])
                        P.op("dve", (lambda t0v=t0v, t3v=t3v, jj=jj: nc.vector.scalar_tensor_tensor(out=QT4[:, jj, 0:ntok], in0=t0v, scalar=128.0 ** -0.5, in1=t3v, op0=ALU.mult, op1=ALU.mult)),
                             reads=[T_b[0], T_b[3]], writes=[QT_b])
                        P.op("dve", (lambda t3v=t3v: nc.vector.reciprocal(out=t3v, in_=t3v)), reads=[T_b[3]], writes=[T_b[3]])
                        P.op("dve", (lambda t1v=t1v, t3v=t3v, jj=jj: nc.vector.tensor_tensor(out=KT4[:, jj, 0:ntok], in0=t1v, in1=t3v, op=ALU.mult)),
                             reads=[T_b[1], T_b[3]], writes=[KT_b])
                        P.op("act", (lambda t2v=t2v, t3v=t3v: nc.scalar.activation(out=t3v, in_=t2v, func=AF.Exp)), reads=[T_b[2], T_b[3]], writes=[T_b[3]])
                        P.op("dve", (lambda t0v=t0v, t3v=t3v, jj=jj: nc.vector.scalar_tensor_tensor(out=QH4[:, jj, 0:ntok], in0=t0v, scalar=128.0 ** -0.5, in1=t3v, op0=ALU.mult, op1=ALU.mult)),
                             reads=[T_b[0], T_b[3]], writes=[QH_b])
                    for ci, (c0, cl, sl) in enumerate(chunks):
                        bk = ci % 4
                        for jj in range(4):
                            P.op("pe", (lambda jj=jj, c0=c0, cl=cl, bk=bk: nc.tensor.transpose(out=PSB16[bk][0:cl, jj * 128:(jj + 1) * 128],
                                                                                              in_=KT4[:, jj, c0:c0 + cl], identity=identbf)),
                                 reads=[KT_b, const_b], writes=[PS_b[bk]], signal=(jj == 3))
                        P.op("dve", (lambda ci=ci, cl=cl, bk=bk: nc.vector.tensor_copy(out=KTT4[0:cl, ci, :], in_=PSB16[bk][0:cl, 0:512])), reads=[PS_b[bk]], writes=[KTT_b])
                    acnt = 0
                    for jj in range(4):
                        h = hg * 4 + jj
                        ob = 4 + jj
                        for ci, (c0, cl, sl) in enumerate(chunks):
                            if kind == "s":
                                r0 = ((jh * c.NSS + sl) * H + h) * 128
                                P.op("sp", (lambda r0=r0, h=h: nc.sync.dma_start(out=S32[:, h, :], in_=state_h[r0:r0 + 128, :])), writes=[S32_b], dma="Dst")
                                P.op("pool", (lambda h=h: nc.gpsimd.tensor_copy(out=SBF[:, h, :], in_=S32[:, h, :])), reads=[S32_b], writes=[SBF_b])
                            ab = acnt % 2
                            pa = acnt % 4
                            acnt += 1
                            P.op("pe", (lambda jj=jj, c0=c0, cl=cl, pa=pa: nc.tensor.matmul(PS[pa][0:cl, 0:cl], lhsT=KT4[:, jj, c0:c0 + cl], rhs=QT4[:, jj, c0:c0 + cl],
                                                                                            start=True, stop=True)),
                                 reads=[KT_b, QT_b], writes=[PS_b[pa]])
                            P.op("dve", (lambda cl=cl, pa=pa, ab=ab: nc.vector.tensor_tensor(out=ATM[ab][0:cl, 0:cl], in0=PS[pa][0:cl, 0:cl], in1=tri32[0:cl, 0:cl], op=ALU.mult)),
                                 reads=[PS_b[pa], const_b], writes=[ATM_b[ab]])
                            P.op("pe", (lambda jj=jj, ci=ci, c0=c0, cl=cl, ab=ab, ob=ob: nc.tensor.matmul(PS[ob][:, c0:c0 + cl], lhsT=VT4[0:cl, ci, jj * 128:(jj + 1) * 128],
                                                                                                        rhs=ATM[ab][0:cl, 0:cl], start=True, stop=False)),
                                 reads=[VT_b, ATM_b[ab]], writes=[PS_b[ob]], signal=False)
                            P.op("pe", (lambda jj=jj, h=h, c0=c0, cl=cl, ob=ob: nc.tensor.matmul(PS[ob][:, c0:c0 + cl], lhsT=SBF[:, h, :], rhs=QH4[:, jj, c0:c0 + cl],
                                                                                                start=False, stop=True)),
                                 reads=[SBF_b, QH_b], writes=[PS_b[ob]])
                            pb = 2 + (acnt % 2)
                            P.op("pe", (lambda jj=jj, ci=ci, cl=cl, pb=pb: nc.tensor.matmul(PS[pb][:, 128:256], lhsT=KTT4[0:cl, ci, jj * 128:(jj + 1) * 128],
                                                                                            rhs=VT4[0:cl, ci, jj * 128:(jj + 1) * 128], start=True, stop=True)),
                                 reads=[KTT_b, VT_b], writes=[PS_b[pb]])
                            P.op("act", (lambda jj=jj, ci=ci, pb=pb, ab=ab: nc.scalar.activation(out=TMP[ab], in_=PS[pb][:, 128:256], func=AF.Copy, scale=ELM[:, jj, ci:ci + 1])),
                                 reads=[PS_b[pb], EL_b], writes=[TMP_b[ab]])
                            P.op("dve", (lambda jj=jj, h=h, ci=ci, ab=ab: nc.vector.scalar_tensor_tensor(out=S32[:, h, :], in0=S32[:, h, :], scalar=EL[:, jj, ci:ci + 1], in1=TMP[ab],
                                                                                                        op0=ALU.mult, op1=ALU.add)),
                                 reads=[S32_b, EL_b, TMP_b[ab]], writes=[S32_b])
                            P.op("pool", (lambda h=h: nc.gpsimd.tensor_copy(out=SBF[:, h, :], in_=S32[:, h, :])), reads=[S32_b], writes=[SBF_b])
                            if kind == "s":
                                r0 = ((jh * c.NSS + sl) * H + h) * 128
                                P.op("act", (lambda r0=r0, h=h: nc.scalar.dma_start(out=o_hs_s[r0:r0 + 128, :], in_=S32[:, h, :])), reads=[S32_b], writes=[hso_b], dma="Dst")
                            elif t0 + ntok == c.SEQ and ci == nch - 1:
                                r0 = ((jh * c.NPS + seq) * H + h) * 128
                                P.op("act", (lambda r0=r0, h=h: nc.scalar.dma_start(out=o_hs_p[r0:r0 + 128, :], in_=S32[:, h, :])), reads=[S32_b], writes=[hso_b], dma="Dst")
                        P.op("act", (lambda jj=jj, ob=ob: nc.scalar.copy(out=O4[:, jj, 0:ntok], in_=PS[ob][:, 0:ntok])), reads=[PS_b[ob]], writes=[O4_b])
                    for jj in range(4):
                        h = hg * 4 + jj
                        P.op("act", (lambda jj=jj: nc.scalar.activation(out=T[0][:, jj, 0:ntok], in_=O4[:, jj, 0:ntok], func=AF.Square)), reads=[O4_b, T_b[0]], writes=[T_b[0]])
                        P.op("pe", (lambda jj=jj: nc.tensor.matmul(PS[3][:, 0:ntok], lhsT=ones32, rhs=T[0][:, jj, 0:ntok], start=True, stop=True)),
                             reads=[T_b[0], const_b], writes=[PS_b[3]])
                        rr = RR[:, 0:ntok]
                        P.op("dve", lambda: nc.vector.tensor_scalar(out=rr, in0=PS[3][:, 0:ntok], scalar1=1.0 / 128, scalar2=RMS_EPS, op0=ALU.mult, op1=ALU.add),
                             reads=[PS_b[3]], writes=[RR_b])
                        P.op("act", lambda: nc.scalar.sqrt(out=rr, in_=rr), reads=[RR_b], writes=[RR_b])
                        P.op("dve", lambda: nc.vector.reciprocal(out=rr, in_=rr), reads=[RR_b], writes=[RR_b])
                        P.op("dve", (lambda jj=jj: nc.vector.tensor_tensor(out=O4[:, jj, 0:ntok], in0=O4[:, jj, 0:ntok], in1=rr, op=ALU.mult)), reads=[O4_b, RR_b], writes=[O4_b])
                        P.op("dve", (lambda jj=jj, h=h: nc.vector.scalar_tensor_tensor(out=ON[:, h, 0:ntok], in0=O4[:, jj, 0:ntok], scalar=SMALL[:, 2:3], in1=GS4[:, jj, 0:ntok],
                                                                                     op0=ALU.mult, op1=ALU.mult)),
                             reads=[O4_b, SMALL_b, GS_b], writes=[ON_b])
                for q in range(4):
                    smm(lambda kc: hgrn_w_out[jh, kc * 128:(kc + 1) * 128, :], KC, q * 512, 512, ON, ON_b, ntok, "n", resid_consumer(xb, ntok, q))
                SQh = v3(o_x32[1], KC, TT)
                layer_norm(xb, ntok, l, 1, sq=SQh, sq_b=T_b[0])
                for i in range(1, 4):
                    T_b[i].rs.extend(T_b[0].rs)
                store_x_to_xs(ti, xb)

            for ti in range(NT):
                run_tile(ti)
            P.barrier_all()

        cur = [0]

        def next_xb():
            cur[0] ^= 1
            return cur[0]

        for l in range(DEPTH):
            for which in range(2):
                s3 = 0 if which == 0 else 2
                for ti in range(NT):
                    kind, seq, t0, ntok = tile_tokens(ti)
                    xb = next_xb()
                    if l == 0 and which == 0:
                        load_x_from_input(ti, xb)
                    else:
                        load_x_from_xs(ti, xb)
                    if not c.skip_ffn:
                        ffn(xb, ntok, l, which)
                    layer_norm(xb, ntok, l, s3)
                    if l == DEPTH - 1 and which == 1:
                        store_x_to_output(ti, xb)
                    else:
                        store_x_to_xs(ti, xb)
                if which == 0 and c.mixers:
                    if l % 2 == 0:
                        if c.only in (None, "fox"):
                            fox_layer(l)
                    else:
                        if c.only in (None, "hgrn"):
                            hgrn_layer(l)

        P.finish(out_bufs)
        P.replay()
    return nc


_NC_CACHE = {}


def kernel(**inputs):
    NCORES = 8
    cfg = Cfg()
    if "nc" not in _NC_CACHE:
        _NC_CACHE["nc"] = build(cfg)
    nc = _NC_CACHE["nc"]
    f = lambda a: np.ascontiguousarray(np.asarray(a, dtype=np.float32))
    xp = f(inputs["x_prompt"]); xsm = f(inputs["x_sample"])
    ck = f(inputs["cache_fox_k"]); cv = f(inputs["cache_fox_v"]); cl = f(inputs["cache_fox_logf"]); sh = f(inputs["state_hgrn"])
    B, SEQ = xp.shape[0], xp.shape[1]
    DB, DS = xsm.shape[0], xsm.shape[1]
    NPS, NSS = B // NCORES, DB // NCORES
    NFOX, NHG = ck.shape[0], sh.shape[0]
    PAST = ck.shape[2]
    shared = {k: f(inputs[k]) for k in ["ln_g", "ln_b", "ffn1_up", "ffn1_down", "ffn2_up", "ffn2_down", "fox_w_in", "fox_b_f",
                                         "fox_w_out", "hgrn_w_in", "hgrn_lb", "hgrn_norm_g", "hgrn_w_out"]}
    in_maps = []
    for cidx in range(NCORES):
        ps, ss = slice(cidx * NPS, (cidx + 1) * NPS), slice(cidx * NSS, (cidx + 1) * NSS)
        m = dict(shared)
        m["x_prompt"] = xp[ps].reshape(NPS * SEQ, D)
        m["x_sample"] = xsm[ss].reshape(NSS * DS, D)
        m["cache_fox_k"] = ck[:, ss].reshape(NFOX * NSS * PAST, D)
        m["cache_fox_v"] = cv[:, ss].reshape(NFOX * NSS * PAST, D)
        m["cache_fox_logf"] = cl[:, ss].reshape(NFOX * NSS * PAST, H)
        m["state_hgrn"] = sh[:, ss].reshape(NHG * NSS * H * 128, 128)
        in_maps.append(m)
    res = run_bass_kernel_spmd(nc, in_maps, core_ids=list(range(NCORES)))
    R = res.results

    def gather(name, shape_core, axis):
        return np.concatenate([np.asarray(R[i][name]).reshape(shape_core) for i in range(NCORES)], axis=axis)
    y_p = gather("y_prompt", (NPS, SEQ, D), 0)
    y_s = gather("y_sample", (NSS, DS, D), 0)
    fk_p = gather("fox_k_prompt", (NFOX, NPS, SEQ, H, 128), 1)
    fv_p = gather("fox_v_prompt", (NFOX, NPS, SEQ, H, 128), 1)
    fl_p = gather("fox_logf_prompt", (NFOX, NPS, SEQ, H), 1)
    hs_p = gather("hgrn_state_prompt", (NHG, NPS, H, 128, 128), 1)
    fk_s = gather("fox_k_sample", (NFOX, NSS, DS, H, 128), 1)
    fv_s = gather("fox_v_sample", (NFOX, NSS, DS, H, 128), 1)
    fl_s = gather("fox_logf_sample", (NFOX, NSS, DS, H), 1)
    hs_s = gather("hgrn_state_sample", (NHG, NSS, H, 128, 128), 1)
    return (y_p, y_s, fk_p, fv_p, fl_p, hs_p, fk_s, fv_s, fl_s, hs_s)
```
